# Optimizing a Trainium2 kernel written in Bass

```python
import jax, jax.numpy as jnp
from jax import lax
import numpy as np

D_MODEL = 2048
BATCH = 16
SEQ = 256
DEPTH = 2
DEC_BATCH = 4
DEC_SEQ = 4096
PAST_LEN = 256

GRID_W = 64
N_HEADS = 8
QK_NOPE_DIM = 128
ROPE_DIM = 64
V_DIM = 128
KV_RANK = 256
ATTN_W = N_HEADS * V_DIM
CONV_W = D_MODEL - ATTN_W
CONV_K = 3
Q_COLS = N_HEADS * (QK_NOPE_DIM + ROPE_DIM)
IN_COLS = Q_COLS + KV_RANK + ROPE_DIM + 3 * CONV_W
D_FF = -(-8 * D_MODEL // (3 * 256)) * 256
Q_BLOCK = 128
ROPE_THETA = 10000.0
EPS = 1e-6

kernel_name = "hybrid_mla_shortconv_diffusion_step"


def _rms_norm(x, g):
    xf = x.astype(jnp.float32)
    y = xf * lax.rsqrt(jnp.mean(xf * xf, axis=-1, keepdims=True) + EPS)
    return (y * g.astype(jnp.float32)).astype(x.dtype)


def _axial_rope(length):
    rows = length // GRID_W
    row = jnp.repeat(jnp.arange(rows, dtype=jnp.float32), GRID_W)
    col = jnp.tile(jnp.arange(GRID_W, dtype=jnp.float32), rows)
    n_freq = ROPE_DIM // 4
    inv = ROPE_THETA ** (-jnp.arange(n_freq, dtype=jnp.float32) / n_freq)
    ang = jnp.concatenate([row[:, None] * inv, col[:, None] * inv], axis=-1)
    return jnp.cos(ang), jnp.sin(ang)


def _apply_rope(x, cos, sin):
    xf = x.astype(jnp.float32)
    x1, x2 = xf[..., :ROPE_DIM // 2], xf[..., ROPE_DIM // 2:]
    return jnp.concatenate([x1 * cos - x2 * sin, x2 * cos + x1 * sin], axis=-1).astype(x.dtype)


def _decompress(ckv, w_uk, w_uv):
    return (jnp.einsum('blr,rhd->blhd', ckv, w_uk),
            jnp.einsum('blr,rhd->blhd', ckv, w_uv))


def _attention(q_nope, q_rope, k_nope, k_rope, v):
    b, lq, h, _ = q_nope.shape
    nb = lq // Q_BLOCK
    scale = (QK_NOPE_DIM + ROPE_DIM) ** -0.5
    qn = q_nope.reshape(b, nb, Q_BLOCK, h, QK_NOPE_DIM).transpose(1, 0, 2, 3, 4)
    qr = q_rope.reshape(b, nb, Q_BLOCK, h, ROPE_DIM).transpose(1, 0, 2, 3, 4)

    def block(args):
        qn_b, qr_b = args
        s = (jnp.einsum('bqhd,bkhd->bhqk', qn_b, k_nope)
             + jnp.einsum('bqhr,bkr->bhqk', qr_b, k_rope))
        p = jax.nn.softmax(s.astype(jnp.float32) * scale, axis=-1).astype(v.dtype)
        return jnp.einsum('bhqk,bkhd->bqhd', p, v)

    o = lax.map(block, (qn, qr))
    return o.transpose(1, 0, 2, 3, 4).reshape(b, lq, h * V_DIM)


def _short_conv(bg, cg, u, w):
    z = cg * u
    zp = jnp.pad(z, ((0, 0), (1, 1), (0, 0)))
    y = zp[:, :-2] * w[0] + zp[:, 1:-1] * w[1] + zp[:, 2:] * w[2]
    return bg * y


def _layer(x, mod, rope, ctx_ckv, ctx_kr, w_in, g_kv, w_uk, w_uv, w_conv,
           g_attn_out, g_conv_out, w_out, g_norm1, g_norm2, w_gate_up, w_down):
    b, l, _ = x.shape
    shift1, scale1, gate1, shift2, scale2, gate2 = jnp.split(mod, 6, axis=-1)
    h = _rms_norm(x, g_norm1) * (1 + scale1) + shift1
    proj = h @ w_in
    o1 = Q_COLS
    o2 = o1 + KV_RANK
    o3 = o2 + ROPE_DIM
    q, ckv, kr, bg, cg, u = jnp.split(proj, [o1, o2, o3, o3 + CONV_W, o3 + 2 * CONV_W], axis=-1)
    ckv = _rms_norm(ckv, g_kv)
    q = q.reshape(b, l, N_HEADS, QK_NOPE_DIM + ROPE_DIM)
    q_nope, q_rope = q[..., :QK_NOPE_DIM], q[..., QK_NOPE_DIM:]
    k_nope, v = _decompress(ckv, w_uk, w_uv)
    if rope is None:
        k_rope = kr
    else:
        cos, sin = rope
        q_rope = _apply_rope(q_rope, cos[None, :, None, :], sin[None, :, None, :])
        kr_lat = _apply_rope(kr, cos[None], sin[None])
        ck_nope, cv = _decompress(ctx_ckv, w_uk, w_uv)
        k_nope = jnp.concatenate([ck_nope, k_nope], axis=1)
        v = jnp.concatenate([cv, v], axis=1)
        k_rope = jnp.concatenate([ctx_kr, kr_lat], axis=1)
    attn = _attention(q_nope, q_rope, k_nope, k_rope, v)
    conv = _short_conv(bg, cg, u, w_conv)
    mixed = jnp.concatenate([_rms_norm(attn, g_attn_out), _rms_norm(conv, g_conv_out)], axis=-1) @ w_out
    x = x + gate1 * mixed
    h2 = _rms_norm(x, g_norm2) * (1 + scale2) + shift2
    gte, up = jnp.split(h2 @ w_gate_up, 2, axis=-1)
    x = x + gate2 * ((jax.nn.silu(gte) * up) @ w_down)
    return x, ckv, kr


def setup_inputs(seed: int = 0) -> dict:
    key = jax.random.key(seed)
    ks = jax.random.split(key, 24)
    f32 = jnp.float32

    def nrm(k, shape, s=1.0):
        return jax.random.normal(k, shape, f32) * s

    def gain(k, shape):
        return 1.0 + 0.05 * jax.random.normal(k, shape, f32)

    return {
        'x_prompt': nrm(ks[0], (BATCH, SEQ, D_MODEL)),
        'x_sample': nrm(ks[1], (DEC_BATCH, DEC_SEQ, D_MODEL)),
        'cache_ckv': nrm(ks[2], (DEC_BATCH, DEPTH, PAST_LEN, KV_RANK)),
        'cache_krope': nrm(ks[3], (DEC_BATCH, DEPTH, PAST_LEN, ROPE_DIM)),
        'c': nrm(ks[4], (DEC_BATCH, D_MODEL)),
        'c_ctx': nrm(ks[5], (D_MODEL,)),
        'w_ada': nrm(ks[6], (DEPTH, D_MODEL, 6 * D_MODEL), 0.5 * D_MODEL ** -0.5),
        'b_ada': nrm(ks[7], (DEPTH, 6 * D_MODEL), 0.02),
        'g_norm1': gain(ks[8], (DEPTH, D_MODEL)),
        'w_in': nrm(ks[9], (DEPTH, D_MODEL, IN_COLS), D_MODEL ** -0.5),
        'g_kv': gain(ks[10], (DEPTH, KV_RANK)),
        'w_uk': nrm(ks[11], (DEPTH, KV_RANK, N_HEADS, QK_NOPE_DIM), KV_RANK ** -0.5),
        'w_uv': nrm(ks[12], (DEPTH, KV_RANK, N_HEADS, V_DIM), KV_RANK ** -0.5),
        'w_conv': nrm(ks[13], (DEPTH, CONV_K, CONV_W), CONV_K ** -0.5),
        'g_attn_out': gain(ks[14], (DEPTH, ATTN_W)),
        'g_conv_out': gain(ks[15], (DEPTH, CONV_W)),
        'w_out': nrm(ks[16], (DEPTH, D_MODEL, D_MODEL), D_MODEL ** -0.5),
        'g_norm2': gain(ks[17], (DEPTH, D_MODEL)),
        'w_gate_up': nrm(ks[18], (DEPTH, D_MODEL, 2 * D_FF), D_MODEL ** -0.5),
        'w_down': nrm(ks[19], (DEPTH, D_FF, D_MODEL), D_FF ** -0.5),
        'g_final': gain(ks[20], (D_MODEL,)),
    }


def reference(x_prompt, x_sample, cache_ckv, cache_krope, c, c_ctx, w_ada, b_ada, g_norm1,
              w_in, g_kv, w_uk, w_uv, w_conv, g_attn_out, g_conv_out, w_out, g_norm2,
              w_gate_up, w_down, g_final):
    rope = _axial_rope(x_sample.shape[1])
    xp, xs = x_prompt, x_sample
    new_ckv, new_kr = [], []
    for l in range(DEPTH):
        mod_ctx = (jax.nn.silu(c_ctx) @ w_ada[l] + b_ada[l])[None, None, :]
        mod_lat = (jax.nn.silu(c) @ w_ada[l] + b_ada[l])[:, None, :]
        ws = (w_in[l], g_kv[l], w_uk[l], w_uv[l], w_conv[l], g_attn_out[l], g_conv_out[l],
              w_out[l], g_norm1[l], g_norm2[l], w_gate_up[l], w_down[l])
        xp, ckv_l, kr_l = _layer(xp, mod_ctx, None, None, None, *ws)
        new_ckv.append(ckv_l)
        new_kr.append(kr_l)
        xs, _, _ = _layer(xs, mod_lat, rope, cache_ckv[:, l], cache_krope[:, l], *ws)
    y_prompt = _rms_norm(xp, g_final)
    y_sample = _rms_norm(xs, g_final)
    new_ckv_arr = jnp.stack(new_ckv, axis=1)
    new_kr_arr = jnp.stack(new_kr, axis=1)
    return (y_prompt, y_sample, new_ckv_arr, new_kr_arr)
```

```python
import math
import os
from contextlib import ExitStack

import numpy as np
import concourse.bass as bass
import concourse.mybir as mybir
from concourse.bass_utils import run_bass_kernel_spmd

F32 = mybir.dt.float32
BF16 = mybir.dt.bfloat16
I32 = mybir.dt.int32
AF = mybir.ActivationFunctionType
ALU = mybir.AluOpType


class Cfg:
    def __init__(self, D=2048, H=8, CW=1024, FF=5632, NPB=2, SP=256, SS=2048, PAST=256,
                 GRID_W=64, DEPTH=2, R=256, GT=512, theta=10000.0, eps=1e-6):
        self.D, self.H, self.CW, self.FF = D, H, CW, FF
        self.NPB, self.SP, self.SS, self.PAST = NPB, SP, SS, PAST
        self.GRID_W, self.DEPTH, self.R, self.GT = GRID_W, DEPTH, R, GT
        self.theta, self.eps = theta, eps
        self.KC = D // 128
        self.CC = CW // 128
        self.FC = FF // 128
        self.NP = NPB * SP
        self.NTOK = self.NP + SS
        self.NT = self.NTOK // 128
        self.QC = H * 192
        self.IN_COLS = self.QC + R + 64 + 3 * CW
        self.NWD = 512 if D % 512 == 0 else D
        self.ND4 = D // self.NWD
        self.FQ = 4 if self.FC % 4 == 0 else 1
        self.FCH = self.FC // self.FQ
        assert D == H * 128 + CW and H % 2 == 0
        assert self.NP % GT == 0 and SS % GT == 0 and GT % 128 == 0 and SP % 128 == 0
        assert PAST % 128 == 0 and R == 256


class Op:
    __slots__ = ("eng", "fn", "deps", "chan", "needed", "val", "bg", "_sem")

    def __init__(self, eng, fn, deps, chan=None, bg=False):
        self.eng, self.fn, self.deps, self.chan = eng, fn, deps, chan
        self.needed = False
        self.val = None
        self.bg = bg
        self._sem = None


class Prog:
    ENG = ("pe", "act", "dve", "pool", "sp")

    def __init__(self, nc, stack):
        self.nc = nc
        self.stack = stack
        self.esem = {e: stack.enter_context(nc.semaphore("e_" + e)) for e in ("pe", "act", "dve", "pool")}
        self.ecnt = {e: 0 for e in self.esem}
        self.chans = {}
        self.ops = []
        self.lastw = {}
        self.readers = {}
        self.bg_done = {}
        self.nblocks = 0

    def _chan(self, name):
        if name not in self.chans:
            self.chans[name] = [self.stack.enter_context(self.nc.semaphore("c_" + name)), 0]
        return self.chans[name]

    def _deps(self, reads, writes):
        deps = []
        for k in reads:
            w = self.lastw.get(k)
            if w is not None:
                deps.append(w)
        for k in writes:
            w = self.lastw.get(k)
            if w is not None:
                deps.append(w)
            deps.extend(self.readers.get(k, ()))
        return deps

    def _record(self, op, reads, writes):
        for k in reads:
            self.readers.setdefault(k, []).append(op)
        for k in writes:
            self.lastw[k] = op
            self.readers[k] = []
        self.ops.append(op)
        return op

    def op(self, eng, fn, reads=(), writes=()):
        return self._record(Op(eng, fn, self._deps(reads, writes)), reads, writes)

    def dma(self, q, out, in_, reads=(), writes=(), chan=None, bg=False, **kw):
        assert chan is not None
        fn = lambda e: e.dma_start(out=out, in_=in_, **kw)
        return self._record(Op(q, fn, self._deps(reads, writes), chan=chan, bg=bg), reads, writes)

    def flush(self, final=False):
        nc = self.nc
        ops = self.ops
        if not ops and not final:
            return
        for o in ops:
            for d in o.deps:
                if isinstance(d, Op):
                    d.needed = True
        per = {e: [] for e in self.ENG}
        fg_dma = {}
        chan_latest = {}
        for o in ops:
            waits = []
            for d in o.deps:
                if isinstance(d, Op):
                    if d.val is None:
                        continue
                    if d.eng == "pe" and o.eng == "pe" and d.chan is None and d._sem is None:
                        continue
                    if d.chan is not None:
                        waits.append(chan_latest[d.chan])
                    else:
                        waits.append(d.val)
                else:
                    waits.append(d)
            o.deps = waits
            if o.val == "coll":
                o.val = (o._sem, 1)
                fg_dma["__c%d" % id(o)] = o.val
            elif o.chan is not None:
                ch = self._chan(o.chan)
                ch[1] += 16
                o.val = (ch[0], ch[1])
                chan_latest[o.chan] = o.val
                if not o.bg:
                    fg_dma[o.chan] = o.val
            elif o.needed:
                self.ecnt[o.eng] += 1
                o.val = (self.esem[o.eng], self.ecnt[o.eng])
            per[o.eng].append(o)
        final_waits = list(fg_dma.values())
        if final:
            final_waits += [(ch[0], ch[1]) for name, ch in self.chans.items() if ch[1] > 0]
        bodies = {}
        for e in self.ENG:
            def body(eng, lst=per[e], ename=e):
                waited = {}
                for o in lst:
                    for v in o.deps:
                        key = id(v[0])
                        if waited.get(key, 0) < v[1]:
                            eng.wait_ge(v[0], v[1])
                            waited[key] = v[1]
                    ins = o.fn(eng)
                    if o.val is not None:
                        if getattr(o, "_sem", None) is not None:
                            ins.then_inc(o.val[0])
                        else:
                            ins.then_inc(o.val[0], 16 if o.chan is not None else 1)
                if ename == "sp":
                    for (s, v) in final_waits:
                        eng.wait_ge(s, v)
            bodies[e] = body
        with nc.Block(no_gpsimd_drain=True) as block:
            block.tensor(bodies["pe"])
            block.scalar(bodies["act"])
            block.vector(bodies["dve"])
            block.gpsimd(bodies["pool"])
            block.sync(bodies["sp"])
        self.nblocks += 1
        newlast = {}
        for k, o in self.lastw.items():
            if isinstance(o, Op) and o.bg and o.val is not None:
                newlast[k] = tuple(chan_latest[o.chan])
            elif not isinstance(o, Op):
                newlast[k] = o
        self.lastw = newlast
        self.readers = {}
        self.ops = []
        self.ucnt = 0

    def uniq(self):
        self.ucnt = getattr(self, "ucnt", 0) + 1
        return "u%d" % self.ucnt

    def retire(self, keys):
        for k in keys:
            self.lastw.pop(k, None)

    def coll(self, fn, reads=(), writes=()):
        sem = self.stack.enter_context(self.nc.semaphore("cc%d" % len(self.chans)))
        self.chans["__cc%d" % len(self.chans)] = [sem, 0]
        o = Op("pool", fn, self._deps(reads, writes))
        o.chan = None
        o.needed = True
        o.val = "coll"
        o._sem = sem
        return self._record(o, reads, writes)


def vchunks(cfg):
    H, CW, QC = cfg.H, cfg.CW, cfg.QC
    out = []
    for h in range(H):
        out.append(("nope", h, [(h * 192, 128, 0)]))
    for i in range(H // 2):
        out.append(("ropeA", i, [((2 * i) * 192 + 128, 64, 0), ((2 * i + 1) * 192 + 128, 64, 64)]))
        segs = []
        for j in range(2):
            b = (2 * i + j) * 192 + 128
            segs.append((b + 32, 32, 64 * j))
            segs.append((b, 32, 64 * j + 32))
        out.append(("ropeB", i, segs))
    o3 = QC + 256 + 64
    for c in range(cfg.CC):
        out.append(("bg", c, [(o3 + c * 128, 128, 0)]))
        out.append(("cg", c, [(o3 + CW + c * 128, 128, 0)]))
        out.append(("u", c, [(o3 + 2 * CW + c * 128, 128, 0)]))
    return out


class _Stop(Exception):
    pass


def build_program(cfg):
    import os
    stop_at = int(os.environ.get("K_STOP", "999"))
    holder = {}
    try:
        _build(cfg, stop_at, holder)
    except _Stop:
        pass
    return holder["nc"]


def _build(cfg, stop_at, holder):
    D, H, CW, FF, KC, CC, FC = cfg.D, cfg.H, cfg.CW, cfg.FF, cfg.KC, cfg.CC, cfg.FC
    NP, SS, NTOK, NT, GT, SP, NPB = cfg.NP, cfg.SS, cfg.NTOK, cfg.NT, cfg.GT, cfg.SP, cfg.NPB
    DEPTH, PAST, NWD, ND4, FQ, FCH = cfg.DEPTH, cfg.PAST, cfg.NWD, cfg.ND4, cfg.FQ, cfg.FCH
    AW = H * 128
    NTS = SS // 128
    TPG = GT // 128
    VC = vchunks(cfg)
    NVC = len(VC)
    nc = bass.Bass("TRN2", target_bir_lowering=False)
    holder["nc"] = nc
    stage = {"n": 0}

    def checkpoint():
        stage["n"] += 1
        if stage["n"] >= stop_at:
            P.flush(final=True)
            raise _Stop()

    def din(name, shape, dt=F32):
        return nc.dram_tensor(name, list(shape), dt, kind="ExternalInput").ap()

    def dout(name, shape, dt=F32):
        return nc.dram_tensor(name, list(shape), dt, kind="ExternalOutput").ap()

    DBG = os.environ.get("K_DBG") == "1"

    def dscr(name, shape, dt=F32):
        if DBG and name in ("XA", "XB", "MODROW", "TQd", "CCd", "SSd"):
            return nc.dram_tensor(name, list(shape), dt, kind="ExternalOutput").ap()
        return nc.dram_tensor(name, list(shape), dt).ap()

    xp = din("xp", [NP, D]); xs = din("xs", [SS, D])
    cckv = din("cckv", [DEPTH, PAST, 256]); ckr = din("ckr", [DEPTH, PAST, 64])
    cvec = din("cvec", [2, D]); pos = din("pos", [SS, 2]); msk = din("msk", [128, 2])
    w_ada = din("w_ada", [DEPTH, D, 6 * D]); b_ada = din("b_ada", [DEPTH, 6 * D])
    g_norm1 = din("g_norm1", [DEPTH, D]); w_in = din("w_in", [DEPTH, D, cfg.IN_COLS])
    g_kv = din("g_kv", [DEPTH, 256]); w_uk = din("w_uk", [DEPTH, 256, AW]); w_uv = din("w_uv", [DEPTH, 256, AW])
    w_conv = din("w_conv", [DEPTH, 3, CW]); g_ao = din("g_attn_out", [DEPTH, AW]); g_co = din("g_conv_out", [DEPTH, CW])
    w_out = din("w_out", [DEPTH, D, D]); g_norm2 = din("g_norm2", [DEPTH, D])
    w_gu = din("w_gate_up", [DEPTH, D, 2 * FF]); w_dn = din("w_down", [DEPTH, FF, D]); g_fin = din("g_final", [D])
    yp = dout("yp", [NP, D]); ys = dout("ys", [SS, D])
    nckv = dout("nckv", [NPB, DEPTH, SP, 256]); nkr = dout("nkr", [NPB, DEPTH, SP, 64])

    MODROW = dscr("MODROW", [DEPTH, 2, 6 * D])
    CCd = dscr("CCd", [128, SS]); SSd = dscr("SSd", [128, SS]); TQd = dscr("TQd", [128, NTS, 128])
    WQI = [dscr("WQI%d" % l, [NVC, 128, KC, 128], BF16) for l in range(DEPTH)]
    WKV = [dscr("WKV%d" % l, [128, KC, 320], BF16) for l in range(DEPTH)]
    WQO = [dscr("WQO%d" % l, [ND4, 128, KC, NWD], BF16) for l in range(DEPTH)]
    WGU = [dscr("WGU%d" % l, [FC, 128, KC, 256], BF16) for l in range(DEPTH)]
    WDN = [dscr("WDN%d" % l, [ND4, FQ, 128, FCH, NWD], BF16) for l in range(DEPTH)]
    XA = dscr("XA", [NTOK, D]); XB = dscr("XB", [NTOK, D])
    ZT = dscr("ZT", [CC, 128, NTOK], BF16); BGT = dscr("BGT", [CC, 128, NTOK], BF16)
    KVP = dscr("KVP", [NP, 320]); NSG = SS // GT
    KVLOCs = [dscr("KVLOC%d" % k, [GT, 320]) for k in range(NSG)]
    KVGs = [dscr("KVG%d" % k, [2 * GT, 320]) for k in range(NSG)]
    EDL = dscr("EDL", [2, CW]); EDG = dscr("EDG", [4, CW])

    DBGT = dout("DBGT", [8, 128, 1280]) if DBG else None
    stack = ExitStack()
    stack.enter_context(nc.allow_non_contiguous_dma(reason="small strided parameter loads"))
    P = Prog(nc, stack)

    uid = {"n": 0}

    def sb(st, name, shape, dt, side=None):
        uid["n"] += 1
        if side is not None:
            return st.enter_context(nc.sbuf_tensor("%s_%d" % (name, uid["n"]), list(shape), dt, side=side))
        return st.enter_context(nc.sbuf_tensor("%s_%d" % (name, uid["n"]), list(shape), dt))

    pcount = {"n": 0}

    def alloc_psum(st, nf, nb):
        pcount["n"] += 1
        f_ = [st.enter_context(nc.psum_tensor("psf%d_%d" % (i, pcount["n"]), [128, 512], F32)) for i in range(nf)]
        b_ = [st.enter_context(nc.psum_tensor("psb%d_%d" % (i, pcount["n"]), [128, 512], BF16)) for i in range(nb)]
        return f_, b_
    PSK = lambda i: ("psf", i)

    ident_b = sb(stack, "ident_b", [128, 128], BF16)
    ident_f = sb(stack, "ident_f", [128, 128], F32)
    ones_b = sb(stack, "ones_b", [128, 128], BF16)
    idx_i = sb(stack, "idx_i", [128, 128], I32)
    mskt = sb(stack, "mskt", [128, 2], F32)
    P.op("pool", lambda e: e.iota(idx_i[:], pattern=[[1, 128]], base=0, channel_multiplier=-1), writes=["idx"])
    P.op("pool", lambda e: e.tensor_scalar(out=ident_b[:], in0=idx_i[:], scalar1=0.0, scalar2=None, op0=ALU.is_equal),
         reads=["idx"], writes=["ident_b"])
    P.op("pool", lambda e: e.tensor_scalar(out=ident_f[:], in0=idx_i[:], scalar1=0.0, scalar2=None, op0=ALU.is_equal),
         reads=["idx"], writes=["ident_f"])
    P.op("pool", lambda e: e.memset(ones_b[:], 1.0), writes=["ones_b"])
    P.dma("sp", mskt[:], msk, writes=["mskt"], chan=P.uniq())

    rr = {"n": 0}

    def ev_engine():
        rr["n"] += 1
        return "act" if rr["n"] % 2 else "dve"

    def copy_op(eng, out, in_, reads, writes):
        if eng == "act":
            return P.op("act", lambda e: e.activation(out=out, in_=in_, func=AF.Copy), reads=reads, writes=writes)
        return P.op(eng, lambda e: e.tensor_copy(out=out, in_=in_), reads=reads, writes=writes)

    def rstd_ops(ssq, r, n_feat, key_in, key_out):
        P.op("dve", lambda e: e.tensor_scalar(out=r, in0=ssq, scalar1=1.0 / n_feat, scalar2=cfg.eps,
                                              op0=ALU.mult, op1=ALU.add), reads=[key_in], writes=[key_out])
        P.op("act", lambda e: e.activation(out=r, in_=r, func=AF.Sqrt), reads=[key_out], writes=[key_out])
        P.op("dve", lambda e: e.reciprocal(out=r, in_=r), reads=[key_out], writes=[key_out])

    def precast_list(l):
        L_ = []
        for vi, (kind, idx, segs) in enumerate(VC):
            for (c0, wd, off) in segs:
                L_.append((WQI[l][vi][:, :, off:off + wd],
                           w_in[l][:, c0:c0 + wd].rearrange("(kc p) j -> p kc j", p=128), ("WQI", l, vi, off), "pc_in"))
        L_.append((WKV[l], w_in[l][:, cfg.QC:cfg.QC + 320].rearrange("(kc p) j -> p kc j", p=128), ("WKV", l), "pc_in"))
        for j in range(ND4):
            L_.append((WQO[l][j], w_out[l][:, j * NWD:(j + 1) * NWD].rearrange("(kc p) j -> p kc j", p=128), ("WQO", l, j), "pc_out"))
        for fc in range(FC):
            L_.append((WGU[l][fc][:, :, 0:128], w_gu[l][:, fc * 128:(fc + 1) * 128].rearrange("(kc p) j -> p kc j", p=128),
                       ("WGU", l, fc, 0), "pc_gu"))
            L_.append((WGU[l][fc][:, :, 128:256], w_gu[l][:, FF + fc * 128:FF + (fc + 1) * 128].rearrange("(kc p) j -> p kc j", p=128),
                       ("WGU", l, fc, 1), "pc_gu"))
        for j in range(ND4):
            for q in range(FQ):
                L_.append((WDN[l][j][q], w_dn[l][q * FCH * 128:(q + 1) * FCH * 128, j * NWD:(j + 1) * NWD].rearrange(
                    "(f p) n -> p f n", p=128), ("WDN", l, j, q), "pc_dn"))
        return L_

    def precast_sel(l, bases, tag):
        for (o_, i_, key, base) in precast_list(l):
            if base in bases:
                P.dma("pool", o_, i_, writes=[key], chan="%s%d_%s" % (base, l, tag), bg=True)

    def precast_part(l, part, nparts):
        L_ = precast_list(l)
        n = len(L_)
        lo, hi = (n * part) // nparts, (n * (part + 1)) // nparts
        for (o_, i_, key, base) in L_[lo:hi]:
            P.dma("pool", o_, i_, writes=[key], chan="%s%d_%d" % (base, l, part), bg=True)

    precast_sel(0, ("pc_in",), "a")
    scT = sb(stack, "scT", [128, KC, 2], BF16)
    ADA_IN_FFN = os.environ.get("K_ADAFFN", "1") == "1"

    with ExitStack() as st:
        psf, psbs = alloc_psum(st, 6, 2)
        cT = sb(st, "cT", [128, KC, 2], F32)
        ADW = 512 if D % 512 == 0 else D
        wad = [sb(st, "wad%d" % i, [128, KC, ADW], BF16) for i in range(2)]
        brow = sb(st, "brow", [2, 6 * D], F32)
        mrow = [sb(st, "mrow%d" % i, [2, ADW], F32) for i in range(2)]
        for m in range(2):
            P.dma("sp", cT[:, :, m], cvec[m].rearrange("(kc p) -> p kc", p=128), writes=["cT"], chan=P.uniq())
        P.op("act", lambda e: e.activation(out=scT[:], in_=cT[:], func=AF.Silu), reads=["cT"], writes=["scT"])
        it = 0
        for l in range(1 if ADA_IN_FFN else DEPTH):
            for m in range(2):
                P.dma("sp", brow[m:m + 1, :], b_ada[l:l + 1, :], writes=[("brow", m)], chan="brow%d" % m)
            for j in range(6 * D // ADW):
                s = it % 2
                it += 1
                P.dma("pool", wad[s][:], w_ada[l][:, j * ADW:(j + 1) * ADW].rearrange("(kc p) n -> p kc n", p=128),
                      writes=[("wad", s)], chan="wad%d" % s)

                def mm(e, s=s):
                    for kc in range(KC):
                        ins = e.matmul(psf[s][0:2, 0:ADW], lhsT=scT[:, kc, :], rhs=wad[s][:, kc, :],
                                       start=(kc == 0), stop=(kc == KC - 1))
                    return ins
                P.op("pe", mm, reads=["scT", ("wad", s)], writes=[PSK(s)])
                P.op("dve", lambda e, s=s, j=j: e.tensor_tensor(out=mrow[s][:], in0=psf[s][0:2, 0:ADW],
                                                                 in1=brow[:, j * ADW:(j + 1) * ADW], op=ALU.add),
                     reads=[PSK(s), ("brow", 0), ("brow", 1)], writes=[("mrow", s)])
                P.dma("sp", MODROW[l][:, j * ADW:(j + 1) * ADW], mrow[s][:], reads=[("mrow", s)],
                      writes=[("MODROW", l, j)], chan="mrow%d" % s)
        P.flush()
        checkpoint()

    with ExitStack() as st:
        psf, psbs = alloc_psum(st, 6, 2)
        post = sb(st, "post", [128, NTS, 2], F32)
        jj = sb(st, "jj", [128, 16], I32)
        jf = sb(st, "jf", [128, 16], F32)
        inv = sb(st, "inv", [128, 16], F32)
        ang = sb(st, "ang", [128, NTS, 32], F32)
        zs = sb(st, "zs", [128, NTS, 32], F32)
        zc = sb(st, "zc", [128, NTS, 32], F32)
        sn = sb(st, "sn", [128, NTS, 32], F32)
        cn = sb(st, "cn", [128, NTS, 32], F32)
        TQ = sb(st, "TQ", [128, NTS, 128], F32)
        TC = sb(st, "TC", [128, NTS, 128], F32)
        TS = sb(st, "TS", [128, NTS, 128], F32)
        CCt = sb(st, "CCt", [128, SS], F32)
        SSt = sb(st, "SSt", [128, SS], F32)
        P.dma("sp", post[:], pos.rearrange("(t p) c -> p t c", p=128), writes=["post"], chan=P.uniq())
        P.op("pool", lambda e: e.iota(jj[:], pattern=[[1, 16]], base=0, channel_multiplier=0), writes=["jj"])
        P.op("dve", lambda e: e.tensor_copy(out=jf[:], in_=jj[:]), reads=["jj"], writes=["jf"])
        P.op("act", lambda e: e.activation(out=inv[:], in_=jf[:], func=AF.Exp, scale=-math.log(cfg.theta) / 16.0),
             reads=["jf"], writes=["inv"])
        for t in range(NTS):
            for a in range(2):
                P.op("dve", lambda e, t=t, a=a: e.tensor_scalar(out=ang[:, t, a * 16:(a + 1) * 16], in0=inv[:],
                                                                scalar1=post[:, t, a:a + 1], scalar2=None,
                                                                op0=ALU.mult),
                     reads=["inv", "post"], writes=["ang"])
        twopi = 2.0 * math.pi
        ki = sb(st, "ki", [128, NTS, 32], I32)
        kf = sb(st, "kf", [128, NTS, 32], F32)
        wv = sb(st, "wv", [128, NTS, 32], F32)

        def range_reduce(src_key, shift, z, zkey, dst, dkey):
            P.op("dve", lambda e: e.tensor_scalar(out=wv[:], in0=ang[:], scalar1=shift, scalar2=1.0 / twopi,
                                                  op0=ALU.add, op1=ALU.mult), reads=["ang"], writes=["wv"])
            P.op("dve", lambda e: e.tensor_copy(out=ki[:], in_=wv[:]), reads=["wv"], writes=["ki"])
            P.op("dve", lambda e: e.tensor_copy(out=kf[:], in_=ki[:]), reads=["ki"], writes=["kf"])
            P.op("dve", lambda e: e.tensor_scalar(out=wv[:], in0=ang[:], scalar1=shift, scalar2=None, op0=ALU.add),
                 reads=["ang", "ki"], writes=["wv"])
            P.op("dve", lambda e: e.scalar_tensor_tensor(out=z[:], in0=kf[:], scalar=-twopi, in1=wv[:],
                                                         op0=ALU.mult, op1=ALU.add), reads=["kf", "wv"], writes=[zkey])
            P.op("dve", lambda e: e.tensor_scalar(out=wv[:], in0=z[:], scalar1=math.pi, scalar2=twopi,
                                                  op0=ALU.is_gt, op1=ALU.mult), reads=[zkey], writes=["wv"])
            P.op("dve", lambda e: e.tensor_tensor(out=z[:], in0=z[:], in1=wv[:], op=ALU.subtract),
                 reads=[zkey, "wv"], writes=[zkey])
            P.op("dve", lambda e: e.tensor_scalar(out=wv[:], in0=z[:], scalar1=-math.pi, scalar2=twopi,
                                                  op0=ALU.is_lt, op1=ALU.mult), reads=[zkey], writes=["wv"])
            P.op("dve", lambda e: e.tensor_tensor(out=z[:], in0=z[:], in1=wv[:], op=ALU.add),
                 reads=[zkey, "wv"], writes=[zkey])
            P.op("act", lambda e: e.activation(out=dst[:], in_=z[:], func=AF.Sin), reads=[zkey], writes=[dkey])
        range_reduce("ang", 0.0, zs, "zs", sn, "sn")
        range_reduce("ang", 0.5 * math.pi, zc, "zc", cn, "cn")
        for q in range(4):
            P.op("dve", lambda e, q=q: e.tensor_copy(out=TC[:, :, q * 32:(q + 1) * 32], in_=cn[:]),
                 reads=["cn"], writes=[("TC", q)])
            if q % 2 == 1:
                P.op("dve", lambda e, q=q: e.tensor_copy(out=TS[:, :, q * 32:(q + 1) * 32], in_=sn[:]),
                     reads=["sn"], writes=[("TS", q)])
            else:
                P.op("dve", lambda e, q=q: e.tensor_scalar(out=TS[:, :, q * 32:(q + 1) * 32], in0=sn[:], scalar1=-1.0,
                                                           scalar2=None, op0=ALU.mult), reads=["sn"], writes=[("TS", q)])
        for q in range(2):
            P.op("dve", lambda e, q=q: e.tensor_copy(out=TQ[:, :, q * 32:(q + 1) * 32], in_=TC[:, :, 0:32]),
                 reads=[("TC", 0)], writes=[("TQ", q)])
            P.op("dve", lambda e, q=q: e.tensor_copy(out=TQ[:, :, 64 + q * 32:64 + (q + 1) * 32],
                                                      in_=TS[:, :, q * 32:(q + 1) * 32]),
                 reads=[("TS", q)], writes=[("TQ", 2 + q)])
        P.dma("sp", TQd, TQ[:], reads=[("TQ", q) for q in range(4)], writes=["TQd"], chan=P.uniq())
        for t in range(NTS):
            for which, (src, dst) in enumerate(((TC, CCt), (TS, SSt))):
                b = (2 * t + which) % 2
                P.op("pe", lambda e, src=src, t=t, b=b: e.transpose(out=psf[b][:, 0:128], in_=src[:, t, :],
                                                                     identity=ident_f[:]),
                     reads=[("TC", q) for q in range(4)] + [("TS", q) for q in range(4)] + ["ident_f"],
                     writes=[PSK(b)])
                copy_op(ev_engine(), dst[:, t * 128:(t + 1) * 128], psf[b][:, 0:128], [PSK(b)], [("CSt", which)])
        P.dma("sp", CCd, CCt[:], reads=[("CSt", 0)], writes=["CCd"], chan=P.uniq())
        P.dma("sp", SSd, SSt[:], reads=[("CSt", 1)], writes=["SSd"], chan=P.uniq())
        P.flush()
        checkpoint()

    groups = [(0, g * GT) for g in range(NP // GT)] + [(1, NP + g * GT) for g in range(SS // GT)]
    groups_A = [g for g in groups if g[0] == 1] + [g for g in groups if g[0] == 0]
    scale_attn = 192.0 ** -0.5

    def bcast_row(src_row):
        return src_row.broadcast_to([128, src_row.shape[-1]])

    def norm_to_hT(st_name, xt_list, keys_x, gmT, shT, m, hT, hkey, xn, junk, ssq, r):
        KN = int(os.environ.get("K_N", "9"))
        for i in range(TPG):
            P.op("act", lambda e, i=i: e.activation(out=junk[:], in_=xt_list[i][:], func=AF.Square,
                                                    accum_out=ssq[:, i:i + 1]),
                 reads=[keys_x[i]], writes=[(st_name, "ssq", i)])
            if KN < 2:
                continue
            rstd_ops(ssq[:, i:i + 1], r[:, i:i + 1], D, (st_name, "ssq", i), (st_name, "r", i))
            if KN < 3:
                continue
            P.op("dve", lambda e, i=i: e.tensor_scalar(out=xn[i][:], in0=xt_list[i][:], scalar1=r[:, i:i + 1],
                                                       scalar2=None, op0=ALU.mult),
                 reads=[keys_x[i], (st_name, "r", i)], writes=[(st_name, "xn", i)])
        if KN < 4:
            return
        for kc in range(KC):
            hb = kc % len(psbs)

            def tr(e, kc=kc, hb=hb):
                for i in range(TPG):
                    ins = e.transpose(out=psbs[hb][:, i * 128:(i + 1) * 128],
                                      in_=xn[i][:, kc * 128:(kc + 1) * 128], identity=ident_b[:])
                return ins
            P.op("pe", tr, reads=[(st_name, "xn", i) for i in range(TPG)] + ["ident_b"], writes=[("psb", hb)])
            if KN < 5:
                continue
            eng = ev_engine() if os.environ.get("K_EVDVE") != "1" else "dve"
            if eng == "act":
                P.op("act", lambda e, kc=kc, hb=hb: e.activation(out=hT[:, kc, :], in_=psbs[hb][:, 0:GT],
                                                                 func=AF.Identity, scale=gmT[:, kc, m:m + 1],
                                                                 bias=shT[:, kc, m:m + 1]),
                     reads=[("psb", hb), "modT"], writes=[hkey])
            else:
                P.op("dve", lambda e, kc=kc, hb=hb: e.tensor_scalar(out=hT[:, kc, :], in0=psbs[hb][:, 0:GT],
                                                                    scalar1=gmT[:, kc, m:m + 1],
                                                                    scalar2=shT[:, kc, m:m + 1],
                                                                    op0=ALU.mult, op1=ALU.add),
                     reads=[("psb", hb), "modT"], writes=[hkey])

    def load_modT(st, l, which_scale, which_shift, gsrc):
        gmT = sb(st, "gmT", [128, KC, 2], F32)
        shT = sb(st, "shT", [128, KC, 2], F32)
        gT = sb(st, "gT", [128, KC], F32)
        P.dma("sp", gT[:], gsrc.rearrange("(c p) -> p c", p=128), writes=["gT"], chan=P.uniq())
        for m in range(2):
            P.dma("sp", gmT[:, :, m], MODROW[l][m, which_scale * D:(which_scale + 1) * D].rearrange("(c p) -> p c", p=128),
                  writes=[("gmraw", m)], chan=P.uniq())
            P.dma("sp", shT[:, :, m], MODROW[l][m, which_shift * D:(which_shift + 1) * D].rearrange("(c p) -> p c", p=128),
                  writes=["modT"], chan=P.uniq())
            P.op("dve", lambda e, m=m: e.scalar_tensor_tensor(out=gmT[:, :, m], in0=gmT[:, :, m], scalar=1.0, in1=gT[:],
                                                              op0=ALU.add, op1=ALU.mult),
                 reads=[("gmraw", m), "gT"], writes=["modT"])
        return gmT, shT

    x_in = None
    for l in range(DEPTH):
        last = (l == DEPTH - 1)
        lst = ExitStack()
        qst = ExitStack()
        if os.environ.get("K_RDUMMY") == "1":
            sb(qst, "rdummy", [128, 30000], BF16, side="right")
        qnT = sb(qst, "qnT", [128, H, NTOK], BF16, side="right")
        qrT = sb(qst, "qrT", [128, H // 2, NTOK], BF16, side="right")

        def xsrc(tok0, n):
            if l == 0:
                return xp[tok0:tok0 + n, :] if tok0 < NP else xs[tok0 - NP:tok0 - NP + n, :]
            return XB[tok0:tok0 + n, :]

        with ExitStack() as st:
            psf, psbs = alloc_psum(st, 6, 2)
            gmT, shT = load_modT(st, l, 1, 0, g_norm1[l])
            xt = [sb(st, "xt%d" % i, [128, D], F32) for i in range(TPG)]
            xn = [sb(st, "xn%d" % i, [128, D], BF16) for i in range(TPG)]
            junk = sb(st, "junk", [128, D], BF16)
            ssq = sb(st, "ssq", [128, 8], F32)
            r = sb(st, "r", [128, 8], F32)
            hT = sb(st, "hT", [128, KC, GT], BF16)
            wsl = [sb(st, "wsl%d" % i, [128, KC, 128], BF16) for i in range(3)]
            wkv = sb(st, "wkv", [128, KC, 320], BF16)
            gkv = sb(st, "gkv", [128, 256], F32)
            TQ = sb(st, "TQa", [128, NTS, 128], F32)
            CCt = sb(st, "CCa", [128, SS], F32)
            SSt = sb(st, "SSa", [128, SS], F32)
            tmpf = [sb(st, "tmpf%d" % i, [128, GT], F32) for i in range(3)]
            zst = [sb(st, "zst%d" % i, [128, GT], BF16) for i in range(2)]
            bgst = [sb(st, "bgst%d" % i, [128, GT], BF16) for i in range(2)]
            kvst = [sb(st, "kvst%d" % i, [128, 320], F32) for i in range(2)]
            kss = sb(st, "kss", [128, 8], F32)
            krk = sb(st, "krk", [128, 8], F32)
            kjunk = sb(st, "kjunk", [128, 256], BF16)
            rt = [sb(st, "rt%d" % i, [128, 64], F32) for i in range(2)]
            zedge = sb(st, "zedge", [128, CC, 2], F32)
            P.dma("sp", wkv[:], WKV[l], reads=[("WKV", l)], writes=["wkv"], chan="wkv")
            P.dma("sp", gkv[:], bcast_row(g_kv[l:l + 1, :]), writes=["gkv"], chan=P.uniq())
            P.dma("sp", TQ[:], TQd, writes=["TQ"], chan=P.uniq())
            P.dma("sp", CCt[:], CCd, writes=["CCt"], chan=P.uniq())
            P.dma("sp", SSt[:], SSd, writes=["SSt"], chan=P.uniq())
            st_ctr = {"bank": 0, "kvi": 0}

            def emit_xload(gm):
                (m_, tk) = gm
                for i in range(TPG):
                    P.dma("sp", xt[i][:], xsrc(tk + i * 128, 128), writes=[("xt", i)], chan="xt%d" % i)

            def emit_stats(gm):
                for i in range(TPG):
                    P.op("act", lambda e, i=i: e.activation(out=junk[:], in_=xt[i][:], func=AF.Square,
                                                            accum_out=ssq[:, i:i + 1]),
                         reads=[("xt", i)], writes=[("A", "ssq", i)])
                    rstd_ops(ssq[:, i:i + 1], r[:, i:i + 1], D, ("A", "ssq", i), ("A", "r", i))
                    P.op("dve", lambda e, i=i: e.tensor_scalar(out=xn[i][:], in0=xt[i][:], scalar1=r[:, i:i + 1],
                                                               scalar2=None, op0=ALU.mult),
                         reads=[("xt", i), ("A", "r", i)], writes=[("A", "xn", i)])

            def emit_transposes(gm):
                (m_, tk) = gm
                for kc in range(KC):
                    hb = kc % 2

                    def tr(e, kc=kc, hb=hb):
                        for i in range(TPG):
                            ins = e.transpose(out=psbs[hb][:, i * 128:(i + 1) * 128],
                                              in_=xn[i][:, kc * 128:(kc + 1) * 128], identity=ident_b[:])
                        return ins
                    P.op("pe", tr, reads=[("A", "xn", i) for i in range(TPG)], writes=[("psb", hb)])
                    if kc % 2 == 0:
                        P.op("act", lambda e, kc=kc, hb=hb: e.activation(out=hT[:, kc, :], in_=psbs[hb][:, 0:GT],
                                                                         func=AF.Identity, scale=gmT[:, kc, m_:m_ + 1],
                                                                         bias=shT[:, kc, m_:m_ + 1]),
                             reads=[("psb", hb), "modT"], writes=[("hT", kc)])
                    else:
                        P.op("dve", lambda e, kc=kc, hb=hb: e.tensor_scalar(out=hT[:, kc, :], in0=psbs[hb][:, 0:GT],
                                                                            scalar1=gmT[:, kc, m_:m_ + 1],
                                                                            scalar2=shT[:, kc, m_:m_ + 1],
                                                                            op0=ALU.mult, op1=ALU.add),
                             reads=[("psb", hb), "modT"], writes=[("hT", kc)])

            HTK = [("hT", kc) for kc in range(KC)]

            def emit_wload(item, slot):
                (gm, vi) = item
                segs = VC[vi][2]
                P.dma("sp", wsl[slot][:], WQI[l][vi], reads=[("WQI", l, vi, off) for (_, _, off) in segs],
                      writes=[("wsl", slot)], chan="wsl%d" % slot)

            def emit_chunk(item, s):
                ((m, tok0), vi) = item
                (kind, idx, segs) = VC[vi]
                s0 = tok0 - NP
                b = st_ctr["bank"] % 6
                st_ctr["bank"] += 1
                stores = []

                def mm(e, s=s, b=b):
                    for kc in range(KC):
                        ins = e.matmul(psf[b][:, 0:GT], lhsT=wsl[s][:, kc, :], rhs=hT[:, kc, :],
                                       start=(kc == 0), stop=(kc == KC - 1))
                    return ins
                P.op("pe", mm, reads=[("wsl", s)] + HTK, writes=[PSK(b)])
                pv = psf[b][:, 0:GT]
                if kind == "nope":
                    copy_op(ev_engine(), qnT[:, idx, tok0:tok0 + GT], pv, [PSK(b)], [("qnT", idx, tok0)])
                elif kind == "ropeA":
                    if m == 0:
                        copy_op(ev_engine(), qrT[:, idx, tok0:tok0 + GT], pv, [PSK(b)], [("qrT", idx, tok0)])
                    else:
                        P.op("dve", lambda e: e.tensor_tensor(out=tmpf[idx % 2][:], in0=pv, in1=CCt[:, s0:s0 + GT], op=ALU.mult),
                             reads=[PSK(b), "CCt"], writes=[("tmpf", idx % 2)])
                elif kind == "ropeB":
                    P.op("dve", lambda e: e.tensor_tensor(out=tmpf[2][:], in0=pv, in1=SSt[:, s0:s0 + GT], op=ALU.mult),
                         reads=[PSK(b), "SSt"], writes=[("tmpf", 2)])
                    P.op("dve", lambda e: e.tensor_tensor(out=qrT[:, idx, tok0:tok0 + GT], in0=tmpf[idx % 2][:], in1=tmpf[2][:], op=ALU.add),
                         reads=[("tmpf", idx % 2), ("tmpf", 2)], writes=[("qrT", idx, tok0)])
                elif kind == "bg":
                    zs_ = idx % 2
                    copy_op("act", bgst[zs_][:], pv, [PSK(b)], [("bgst", zs_)])
                    stores.append(lambda: P.dma("sp", BGT[idx][:, tok0:tok0 + GT], bgst[zs_][:], reads=[("bgst", zs_)],
                                                writes=[("BGT", idx, tok0)], chan="bgst%d" % zs_))
                elif kind == "cg":
                    copy_op("act", tmpf[idx % 2][:], pv, [PSK(b)], [("tmpf", idx % 2)])
                elif kind == "u":
                    zs_ = idx % 2
                    P.op("dve", lambda e: e.tensor_tensor(out=zst[zs_][:], in0=tmpf[idx % 2][:], in1=pv, op=ALU.mult),
                         reads=[PSK(b), ("tmpf", idx % 2)], writes=[("zst", zs_)])
                    if m == 1 and s0 == 0:
                        P.op("dve", lambda e: e.tensor_copy(out=zedge[:, idx, 0:1], in_=zst[zs_][:, 0:1]),
                             reads=[("zst", zs_)], writes=["zedge"])
                    if m == 1 and s0 + GT == SS:
                        P.op("dve", lambda e: e.tensor_copy(out=zedge[:, idx, 1:2], in_=zst[zs_][:, GT - 1:GT]),
                             reads=[("zst", zs_)], writes=["zedge"])
                    stores.append(lambda: P.dma("sp", ZT[idx][:, tok0:tok0 + GT], zst[zs_][:], reads=[("zst", zs_)],
                                                writes=[("ZT", idx, tok0)], chan="zst%d" % zs_))
                return stores

            def emit_kv(gm):
                (m, tok0) = gm
                s0 = tok0 - NP
                for i in range(TPG):
                    b = st_ctr["bank"] % 6
                    st_ctr["bank"] += 1
                    ks = st_ctr["kvi"] % 2
                    st_ctr["kvi"] += 1
                    t0 = tok0 + i * 128

                    def mmkv(e, i=i, b=b):
                        for kc in range(KC):
                            ins = e.matmul(psf[b][:, 0:320], lhsT=hT[:, kc, i * 128:(i + 1) * 128], rhs=wkv[:, kc, :],
                                           start=(kc == 0), stop=(kc == KC - 1))
                        return ins
                    P.op("pe", mmkv, reads=HTK + ["wkv"], writes=[PSK(b)])
                    P.op("act", lambda e, b=b, ks=ks: e.activation(out=kjunk[:], in_=psf[b][:, 0:256], func=AF.Square,
                                                                   accum_out=kss[:, ks:ks + 1]),
                         reads=[PSK(b)], writes=[("kss", ks)])
                    rstd_ops(kss[:, ks:ks + 1], krk[:, ks:ks + 1], 256, ("kss", ks), ("krk", ks))
                    P.op("dve", lambda e, b=b, ks=ks: e.scalar_tensor_tensor(
                        out=kvst[ks][:, 0:256], in0=psf[b][:, 0:256], scalar=krk[:, ks:ks + 1], in1=gkv[:],
                        op0=ALU.mult, op1=ALU.mult), reads=[PSK(b), ("krk", ks), "gkv"], writes=[("kvst", ks, 0)])
                    if m == 0:
                        copy_op("dve", kvst[ks][:, 256:320], psf[b][:, 256:320], [PSK(b), ("kvst", ks, 0)], [("kvst", ks, 1)])
                    else:
                        ti = (s0 + i * 128) // 128
                        P.op("dve", lambda e, b=b, ks=ks, ti=ti: e.tensor_tensor(
                            out=kvst[ks][:, 256:320], in0=psf[b][:, 256:320], in1=TQ[:, ti, 0:64], op=ALU.mult),
                            reads=[PSK(b), "TQ", ("kvst", ks, 0)], writes=[("kvst", ks, 1)])
                        P.op("dve", lambda e, b=b, ks=ks, ti=ti: e.tensor_tensor(
                            out=rt[ks][:, 0:32], in0=psf[b][:, 288:320], in1=TQ[:, ti, 64:96], op=ALU.mult),
                            reads=[PSK(b), "TQ", ("kvst", ks, 0)], writes=[("rt", ks, 0)])
                        P.op("dve", lambda e, b=b, ks=ks, ti=ti: e.tensor_tensor(
                            out=rt[ks][:, 32:64], in0=psf[b][:, 256:288], in1=TQ[:, ti, 96:128], op=ALU.mult),
                            reads=[PSK(b), "TQ", ("kvst", ks, 0)], writes=[("rt", ks, 1)])
                        P.op("dve", lambda e, ks=ks: e.tensor_tensor(
                            out=kvst[ks][:, 256:320], in0=kvst[ks][:, 256:320], in1=rt[ks][:], op=ALU.add),
                            reads=[("kvst", ks, 1), ("rt", ks, 0), ("rt", ks, 1)], writes=[("kvst", ks, 1)])
                    rd = [("kvst", ks, 0), ("kvst", ks, 1)]
                    if m == 0:
                        bi, so = t0 // SP, t0 % SP
                        P.dma("sp", nckv[bi, l, so:so + 128, :], kvst[ks][:, 0:256], reads=rd, writes=[("nckv", t0)], chan="kvo%d" % ks)
                        P.dma("sp", nkr[bi, l, so:so + 128, :], kvst[ks][:, 256:320], reads=rd, writes=[("nkr", t0)], chan="kvo%d" % ks)
                        P.dma("sp", KVP[t0:t0 + 128, :], kvst[ks][:], reads=rd, writes=[("KVP", t0)], chan="kvo%d" % ks)
                    else:
                        sg_, so_ = (t0 - NP) // GT, (t0 - NP) % GT
                        P.dma("sp", KVLOCs[sg_][so_:so_ + 128, :], kvst[ks][:], reads=rd, writes=[("KVLOC", t0)], chan="kvo%d" % ks)

            seq = []
            for gm in groups_A:
                for vi, (kind, idx, segs) in enumerate(VC):
                    if kind == "ropeB" and gm[0] == 0:
                        continue
                    seq.append((gm, vi))
            NSL = 3
            for j in range(min(NSL, len(seq))):
                emit_wload(seq[j], j % NSL)
            emit_xload(groups_A[0])
            emit_stats(groups_A[0])
            emit_transposes(groups_A[0])
            pos_in_seq = 0
            for gi_, gm in enumerate(groups_A):
                items = [it_ for it_ in seq if it_[0] == gm]
                nxt = groups_A[gi_ + 1] if gi_ + 1 < len(groups_A) else None
                for ci, item in enumerate(items):
                    slot = pos_in_seq % NSL
                    stores = emit_chunk(item, slot)
                    if pos_in_seq + NSL < len(seq):
                        emit_wload(seq[pos_in_seq + NSL], slot)
                    for th in stores:
                        th()
                    pos_in_seq += 1
                    if nxt is not None and ci == 2:
                        emit_xload(nxt)
                    if nxt is not None and ci == len(items) // 2:
                        emit_stats(nxt)
                emit_kv(gm)
                if gm[0] == 1 and gm[1] - NP + GT == SS:
                    for ee in range(2):
                        P.dma("sp", EDL[ee].rearrange("(c p) -> p c", p=128), zedge[:, :, ee], reads=["zedge"], writes=[("EDL", ee)], chan=P.uniq())
                    RG = [[2 * i, 2 * i + 1] for i in range(4)]
                    for k in range(NSG):
                        P.coll(lambda e, k=k: e.collective_compute("AllGather", ALU.bypass, replica_groups=RG,
                                                                   ins=[KVLOCs[k]], outs=[KVGs[k]]),
                               reads=[("KVLOC", NP + k * GT + i * 128) for i in range(TPG)], writes=[("KVG", k)])
                    P.coll(lambda e: e.collective_compute("AllGather", ALU.bypass, replica_groups=RG,
                                                          ins=[EDL], outs=[EDG]), reads=[("EDL", 0), ("EDL", 1)], writes=["EDG"])
                if nxt is not None:
                    emit_transposes(nxt)
            P.flush()
            checkpoint()

        OTg = sb(lst, "OTg", [128, H, NTOK], BF16)
        ra = sb(lst, "ra", [128, NT], F32)
        rc = sb(lst, "rc", [128, NT], F32)
        ssacc = sb(lst, "ssacc", [128, NT], F32)
        cacc = sb(lst, "cacc", [128, NT], F32)
        NKMAX = PAST + 2 * SS
        with ExitStack() as st:
            psf, psbs = alloc_psum(st, 7, 1)
            ckvT = sb(st, "ckvT", [128, 2, NKMAX], BF16)
            krT2 = sb(st, "krT2", [128, NKMAX], BF16)
            KnT = [sb(st, "KnT%d" % i, [128, NKMAX], BF16) for i in range(2)]
            Vt = [sb(st, "Vt%d" % i, [128, NKMAX // 128, 128], BF16) for i in range(2)]
            wuk = sb(st, "wuk", [128, 2, AW], BF16)
            wuv = sb(st, "wuv", [128, 2, AW], BF16)
            kvt = [sb(st, "kvt%d" % i, [128, 320], F32) for i in range(2)]
            kvb = [sb(st, "kvb%d" % i, [128, 384], BF16) for i in range(2)]
            PT = [sb(st, "PT%d" % i, [128, GT], BF16) for i in range(3)]
            rec = sb(st, "rec", [128, GT], F32)
            otmp = sb(st, "otmp", [128, GT], F32)
            sq = sb(st, "sq", [128, GT], BF16)
            gaT = sb(st, "gaT", [128, H], F32)
            accs = [sb(st, "acc%d" % i, [128, GT], F32) for i in range(2)]
            acc_hi = sb(st, "acc_hi", [128, GT], BF16)
            acc_lo = sb(st, "acc_lo", [128, GT], BF16)
            acc_r = sb(st, "acc_r", [128, GT], F32)
            itc = 0
            P.dma("pool", wuk[:], w_uk[l].rearrange("(c p) n -> p c n", p=128), writes=["wuk"], chan="wuk")
            P.dma("pool", wuv[:], w_uv[l].rearrange("(c p) n -> p c n", p=128), writes=["wuv"], chan="wuv")
            P.dma("sp", gaT[:], g_ao[l].rearrange("(c p) -> p c", p=128), writes=["gaT"], chan=P.uniq())
            if l == 0:
                precast_sel(0, ("pc_out", "pc_gu", "pc_dn"), "b")
            if l + 1 < DEPTH:
                precast_sel(l + 1, ("pc_in", "pc_out", "pc_gu", "pc_dn"), "a")
            if os.environ.get("K_ATT") == "0":
                P.flush(final=True)
                raise _Stop()
            seqs = []
            for bi in range(NPB):
                seqs.append((bi * SP, SP, [(KVP[bi * SP + t * 128: bi * SP + (t + 1) * 128, :], None) for t in range(SP // 128)]))
            srcs = [(cckv[l][t * 128:(t + 1) * 128, :], ckr[l][t * 128:(t + 1) * 128, :]) for t in range(PAST // 128)]
            for k in range(NSG):
                srcs += [(KVGs[k][t * 128:(t + 1) * 128, :], None) for t in range(2 * GT // 128)]
            seqs.append((NP, SS, srcs))
            li = 0
            pti = 0
            hi = 0
            for (q0, nq, srcs) in seqs:
                nkt = len(srcs)
                NK = nkt * 128
                for kt, (a, b2) in enumerate(srcs):
                    s = li % 2
                    li += 1
                    if b2 is None:
                        P.dma("sp", kvt[s][:], a, reads=[("KVG", k_) for k_ in range(NSG)], writes=[("kvt", s)], chan="kvt%d" % s)
                    else:
                        P.dma("sp", kvt[s][:, 0:256], a, writes=[("kvt", s)], chan="kvt%d" % s)
                        P.dma("sp", kvt[s][:, 256:320], b2, writes=[("kvt", s)], chan="kvt%d" % s)
                    P.op("act", lambda e, s=s: e.activation(out=kvb[s][:, 0:320], in_=kvt[s][:], func=AF.Copy),
                         reads=[("kvt", s)], writes=[("kvb", s)])
                    P.op("dve", lambda e, s=s: e.tensor_copy(out=kvb[s][:, 320:384], in_=kvt[s][:, 256:320]),
                         reads=[("kvt", s)], writes=[("kvb", s)])
                    hb = 0

                    def trk(e, s=s, hb=hb):
                        for j in range(int(os.environ.get("K_TRJ", "3"))):
                            ins = e.transpose(out=psbs[hb][:, j * 128:(j + 1) * 128],
                                              in_=kvb[s][:, j * 128:(j + 1) * 128], identity=ident_b[:])
                        return ins
                    P.op("pe", trk, reads=[("kvb", s), "ident_b"], writes=[("psb", hb)])
                    if os.environ.get("K_NOEV") == "1":
                        continue
                    if os.environ.get("K_NOEV") != "3":
                        P.op("dve", lambda e, kt=kt, hb=hb: e.tensor_copy(
                            out=ckvT[:, :, kt * 128:(kt + 1) * 128],
                            in_=psbs[hb][:, 0:256].rearrange("p (a b) -> p a b", b=128)),
                            reads=[("psb", hb)], writes=["ckvT", ("evd", hb)])
                    if os.environ.get("K_NOEV") == "2":
                        continue
                    P.op("act", lambda e, kt=kt, hb=hb: e.activation(out=krT2[:, kt * 128:(kt + 1) * 128],
                                                                      in_=psbs[hb][:, 256:384], func=AF.Copy),
                         reads=[("psb", hb), ("evd", hb)], writes=["krT2"])
                KA = int(os.environ.get("K_ATT", "9"))
                if KA == 1:
                    P.flush(final=True)
                    raise _Stop()
                qgs = [(q0 + j, min(GT, nq - j)) for j in range(0, nq, GT)]
                for h in range(H):
                    hs = hi % 2
                    hi += 1
                    for k0 in range(0, NK, 512):
                        kw = min(512, NK - k0)

                        def mk(e, k0=k0, kw=kw, h=h):
                            for c2 in range(2):
                                ins = e.matmul(psf[5][:, 0:kw], lhsT=wuk[:, c2, h * 128:(h + 1) * 128],
                                               rhs=ckvT[:, c2, k0:k0 + kw], start=(c2 == 0), stop=(c2 == 1))
                            return ins
                        P.op("pe", mk, reads=["wuk", "ckvT"], writes=[PSK(5)])
                        copy_op(ev_engine(), KnT[hs][:, k0:k0 + kw], psf[5][:, 0:kw], [PSK(5)], [("KnT", hs)])
                    for k0 in range(0, nkt, 4):
                        kn = min(4, nkt - k0)

                        def mv(e, k0=k0, kn=kn, h=h):
                            for t in range(kn):
                                for c2 in range(2):
                                    ins = e.matmul(psf[5][:, t * 128:(t + 1) * 128],
                                                   lhsT=ckvT[:, c2, (k0 + t) * 128:(k0 + t + 1) * 128],
                                                   rhs=wuv[:, c2, h * 128:(h + 1) * 128], start=(c2 == 0), stop=(c2 == 1))
                            return ins
                        P.op("pe", mv, reads=["wuv", "ckvT"], writes=[PSK(5)])
                        copy_op(ev_engine(), Vt[hs][:, k0:k0 + kn, :],
                                psf[5][:, 0:kn * 128].rearrange("p (a b) -> p a b", b=128), [PSK(5)], [("Vt", hs)])
                    pr = (h % 2) * 64
                    if KA == 2:
                        P.flush(final=True)
                        raise _Stop()
                    def attn_iter(qa, qn, h=h, hs=hs, pr=pr, nkt=nkt, NK=NK):
                        nonlocal pti, itc
                        po, pl = 2 + (itc % 2), 4
                        ai = itc % 2
                        itc += 1
                        acc = accs[ai]

                        def qk(e, kt, sbk, qa=qa, qn=qn, h=h, hs=hs, pr=pr):
                            e.matmul(psf[sbk][:, 0:qn], lhsT=KnT[hs][:, kt * 128:(kt + 1) * 128], rhs=qnT[:, h, qa:qa + qn],
                                     start=True, stop=False)
                            return e.matmul(psf[sbk][:, 0:qn], lhsT=krT2[pr:pr + 64, kt * 128:(kt + 1) * 128],
                                            rhs=qrT[pr:pr + 64, h // 2, qa:qa + qn], start=False, stop=True)
                        qdeps = [("KnT", hs), "krT2"] + [("qnT", h, qa), ("qrT", h // 2, qa)]
                        ptl = []

                        def emit_pv(k2):
                            p2 = ptl[k2]

                            def pv(e, k2=k2, p2=p2):
                                e.matmul(psf[po][:, 0:qn], lhsT=Vt[hs][:, k2, :], rhs=PT[p2][:, 0:qn],
                                         start=(k2 == 0), stop=(k2 == nkt - 1))
                                return e.matmul(psf[pl][:, 0:qn], lhsT=ones_b[:], rhs=PT[p2][:, 0:qn],
                                                start=(k2 == 0), stop=(k2 == nkt - 1))
                            P.op("pe", pv, reads=[("Vt", hs), ("PT", p2), "ones_b"], writes=[PSK(po), PSK(pl)])
                        for kt in range(nkt):
                            sbk = kt % 2
                            P.op("pe", lambda e, kt=kt, sbk=sbk, f=qk: f(e, kt, sbk), reads=qdeps, writes=[PSK(sbk)])
                            ps_ = pti % 3
                            pti += 1
                            P.op("act", lambda e, sbk=sbk, ps_=ps_: e.activation(
                                out=PT[ps_][:, 0:qn], in_=psf[sbk][:, 0:qn], func=AF.Exp, scale=scale_attn),
                                reads=[PSK(sbk)], writes=[("PT", ps_)])
                            ptl.append(ps_)
                            if kt >= 1:
                                emit_pv(kt - 1)
                        emit_pv(nkt - 1)
                        P.op("dve", lambda e: e.tensor_copy(out=acc_r[:, 0:qn], in_=psf[pl][:, 0:qn]), reads=[PSK(pl)], writes=["acc_r"])
                        if KA == 3:
                            P.flush(final=True)
                            raise _Stop()
                        P.op("dve", lambda e, qn=qn: e.reciprocal(out=rec[:, 0:qn], in_=acc_r[:, 0:qn]),
                             reads=["acc_r"], writes=["rec"])
                        P.op("dve", lambda e, qn=qn: e.tensor_tensor(out=otmp[:, 0:qn], in0=psf[po][:, 0:qn],
                                                                     in1=rec[:, 0:qn], op=ALU.mult),
                             reads=[PSK(po), "rec"], writes=["otmp"])
                        if DBG and l == 0 and h == 0 and qa == NP:
                            P.dma("pool", DBGT[0][:, 0:qn], otmp[:, 0:qn], reads=["otmp"], writes=["DBGT"], chan="dbg")
                            P.dma("pool", DBGT[1][:, 0:NK], ckvT[:, 0, 0:NK], reads=["ckvT"], writes=["DBGT"], chan="dbg")
                            P.dma("pool", DBGT[2][:, 0:NK], krT2[:, 0:NK], reads=["krT2"], writes=["DBGT"], chan="dbg")
                            P.dma("pool", DBGT[3][:, 0:qn], qnT[:, 0, qa:qa + qn], writes=["DBGT"], chan="dbg")
                            P.dma("pool", DBGT[4][:, 0:qn], qrT[:, 0, qa:qa + qn], writes=["DBGT"], chan="dbg")
                            P.dma("pool", DBGT[5][:, 0:NK], KnT[hs][:, 0:NK], reads=[("KnT", hs)], writes=["DBGT"], chan="dbg")
                            P.dma("pool", DBGT[6][:, 0:NK], Vt[hs][:, 0:NK // 128, :], reads=[("Vt", hs)], writes=["DBGT"], chan="dbg")
                        P.op("act", lambda e, qn=qn, qa=qa, h=h: e.activation(out=OTg[:, h, qa:qa + qn], in_=otmp[:, 0:qn],
                                                                             func=AF.Copy, scale=gaT[:, h:h + 1]),
                             reads=["otmp", "gaT"], writes=[("OTg", qa)])
                        P.op("act", lambda e, qn=qn: e.activation(out=sq[:, 0:qn], in_=otmp[:, 0:qn], func=AF.Square),
                             reads=["otmp"], writes=["sq"])

                        def ssm(e, qn=qn, qa=qa, h=h):
                            for t in range(qn // 128):
                                col = (qa // 128) + t
                                ins = e.matmul(psf[6][:, col:col + 1], lhsT=sq[:, t * 128:(t + 1) * 128],
                                               rhs=ones_b[:, 0:1], start=True, stop=True)
                            return ins
                        P.op("pe", ssm, reads=["sq", "ones_b"], writes=[("pss",)])
                        ca, cb = qa // 128, (qa + qn) // 128
                        if h == 0:
                            P.op("dve", lambda e, ca=ca, cb=cb: e.tensor_copy(out=ssacc[:, ca:cb], in_=psf[6][:, ca:cb]),
                                 reads=[("pss",)], writes=["ssacc"])
                        else:
                            P.op("dve", lambda e, ca=ca, cb=cb: e.tensor_tensor(out=ssacc[:, ca:cb], in0=ssacc[:, ca:cb],
                                                                                in1=psf[6][:, ca:cb], op=ALU.add),
                                 reads=[("pss",), "ssacc"], writes=["ssacc"])
                        if KA == 5:
                            P.flush(final=True)
                            raise _Stop()
                    for (qa_, qn_) in qgs:
                        attn_iter(qa_, qn_)
                c0, c1 = q0 // 128, (q0 + nq) // 128
                P.op("dve", lambda e, c0=c0, c1=c1: e.tensor_scalar(out=ra[:, c0:c1], in0=ssacc[:, c0:c1],
                                                                    scalar1=1.0 / AW, scalar2=cfg.eps, op0=ALU.mult, op1=ALU.add),
                     reads=["ssacc"], writes=["ra"])
                P.op("act", lambda e, c0=c0, c1=c1: e.activation(out=ra[:, c0:c1], in_=ra[:, c0:c1], func=AF.Sqrt),
                     reads=["ra"], writes=["ra"])
                P.op("dve", lambda e, c0=c0, c1=c1: e.reciprocal(out=ra[:, c0:c1], in_=ra[:, c0:c1]),
                     reads=["ra"], writes=["ra"])
            P.flush()
            checkpoint()

        qst.close()
        CTg = sb(lst, "CTg", [128, CC, NTOK], BF16)
        with ExitStack() as st:
            psf, psbs = alloc_psum(st, 6, 2)
            LM = max(SP, SS)
            NCB = 3
            zts = [sb(st, "zt%d" % i, [128, LM + 2], BF16) for i in range(NCB)]
            bgts = [sb(st, "bgt%d" % i, [128, LM], BF16) for i in range(NCB)]
            y1s = [sb(st, "y1_%d" % i, [128, LM], F32) for i in range(NCB)]
            y2s = [sb(st, "y2_%d" % i, [128, LM], F32) for i in range(NCB)]
            sqcs = [sb(st, "sqc%d" % i, [128, LM], BF16) for i in range(NCB)]
            wcT = sb(st, "wcT", [128, CC, 3], F32)
            gcT = sb(st, "gcT", [128, CC], F32)
            edg = sb(st, "edg", [128, CC, 2], F32)
            for kk in range(3):
                P.dma("sp", wcT[:, :, kk], w_conv[l][kk].rearrange("(c p) -> p c", p=128), writes=["wcT"], chan=P.uniq())
            P.dma("sp", gcT[:], g_co[l].rearrange("(c p) -> p c", p=128), writes=["gcT"], chan=P.uniq())
            P.dma("sp", edg[:, :, 0], EDG[1].rearrange("(c p) -> p c", p=128), reads=["EDG"], writes=["edg"], chan=P.uniq())
            P.dma("sp", edg[:, :, 1], EDG[2].rearrange("(c p) -> p c", p=128), reads=["EDG"], writes=["edg"], chan=P.uniq())
            for a_ in range(2):
                P.op("dve", lambda e, a_=a_: e.tensor_scalar(out=edg[:, :, a_], in0=edg[:, :, a_], scalar1=mskt[:, a_:a_ + 1],
                                                             scalar2=None, op0=ALU.mult), reads=["edg", "mskt"], writes=["edg"])
            cseqs = [(bi * SP, SP, False) for bi in range(NPB)] + [(NP, SS, True)]

            def conv_s1(t0, L, is_s, c, k):
                zt, y1 = zts[k], y1s[k]
                P.dma("sp", zt[:, 1:L + 1], ZT[c][:, t0:t0 + L], writes=[("zt", k)], chan="zt%d" % k)
                P.dma("sp", bgts[k][:, 0:L], BGT[c][:, t0:t0 + L], writes=[("bgt", k)], chan="bgt%d" % k)
                if is_s:
                    P.op("dve", lambda e: e.tensor_copy(out=zt[:, 0:1], in_=edg[:, c, 0:1]), reads=["edg", ("zt", k)], writes=[("zth", k)])
                    P.op("dve", lambda e: e.tensor_copy(out=zt[:, L + 1:L + 2], in_=edg[:, c, 1:2]), reads=["edg", ("zt", k)], writes=[("zth", k)])
                else:
                    P.op("dve", lambda e: e.memset(zt[:, 0:1], 0.0), reads=[("zt", k)], writes=[("zth", k)])
                    P.op("dve", lambda e: e.memset(zt[:, L + 1:L + 2], 0.0), reads=[("zt", k)], writes=[("zth", k)])
                P.op("act", lambda e: e.activation(out=y1[:, 0:L], in_=zt[:, 1:L + 1], func=AF.Copy, scale=wcT[:, c, 1:2]),
                     reads=[("zt", k), "wcT"], writes=[("y1", k)])

            def conv_s2(t0, L, is_s, c, k):
                zt, bgt, y1, y2 = zts[k], bgts[k], y1s[k], y2s[k]
                P.op("dve", lambda e: e.scalar_tensor_tensor(out=y1[:, 0:L], in0=zt[:, 0:L], scalar=wcT[:, c, 0:1], in1=y1[:, 0:L],
                                                             op0=ALU.mult, op1=ALU.add),
                     reads=[("zt", k), ("zth", k), "wcT", ("y1", k)], writes=[("y1", k)])
                P.op("dve", lambda e: e.scalar_tensor_tensor(out=y1[:, 0:L], in0=zt[:, 2:L + 2], scalar=wcT[:, c, 2:3], in1=y1[:, 0:L],
                                                             op0=ALU.mult, op1=ALU.add),
                     reads=[("zt", k), ("zth", k), "wcT", ("y1", k)], writes=[("y1", k)])
                P.op("dve", lambda e: e.tensor_tensor(out=y2[:, 0:L], in0=y1[:, 0:L], in1=bgt[:, 0:L], op=ALU.mult),
                     reads=[("y1", k), ("bgt", k)], writes=[("y2", k)])

            def conv_s3(t0, L, is_s, c, k):
                y2, sqc = y2s[k], sqcs[k]
                P.op("act", lambda e: e.activation(out=CTg[:, c, t0:t0 + L], in_=y2[:, 0:L], func=AF.Copy, scale=gcT[:, c:c + 1]),
                     reads=[("y2", k), "gcT"], writes=[("CTg", c, t0)])
                P.op("act", lambda e: e.activation(out=sqc[:, 0:L], in_=y2[:, 0:L], func=AF.Square),
                     reads=[("y2", k)], writes=[("sqc", k)])

                def ssc(e):
                    for t in range(L // 128):
                        col = t0 // 128 + t
                        ins = e.matmul(psf[2][:, 256 + col:256 + col + 1], lhsT=sqc[:, t * 128:(t + 1) * 128],
                                       rhs=ones_b[:, 0:1], start=True, stop=True)
                    return ins
                P.op("pe", ssc, reads=[("sqc", k), "ones_b"], writes=[("pssc",)])
                ca, cb = t0 // 128, (t0 + L) // 128
                if c == 0:
                    P.op("dve", lambda e: e.tensor_copy(out=cacc[:, ca:cb], in_=psf[2][:, 256 + ca:256 + cb]),
                         reads=[("pssc",)], writes=["cacc"])
                else:
                    P.op("dve", lambda e: e.tensor_tensor(out=cacc[:, ca:cb], in0=cacc[:, ca:cb],
                                                          in1=psf[2][:, 256 + ca:256 + cb], op=ALU.add),
                         reads=[("pssc",), "cacc"], writes=["cacc"])
            work = [(t0, L, is_s, c) for (t0, L, is_s) in cseqs for c in range(CC)]
            nw = len(work)
            for i in range(nw + 2):
                if i < nw:
                    conv_s1(*work[i], i % NCB)
                if 0 <= i - 1 < nw:
                    conv_s2(*work[i - 1], (i - 1) % NCB)
                if 0 <= i - 2 < nw:
                    conv_s3(*work[i - 2], (i - 2) % NCB)
            P.op("dve", lambda e: e.tensor_scalar(out=rc[:], in0=cacc[:], scalar1=1.0 / CW, scalar2=cfg.eps,
                                                  op0=ALU.mult, op1=ALU.add), reads=["cacc"], writes=["rc"])
            P.op("act", lambda e: e.activation(out=rc[:], in_=rc[:], func=AF.Sqrt), reads=["rc"], writes=["rc"])
            P.op("dve", lambda e: e.reciprocal(out=rc[:], in_=rc[:]), reads=["rc"], writes=["rc"])
            P.flush()
            checkpoint()

        with ExitStack() as st:
            psf, psbs = alloc_psum(st, 6, 2)
            wo = [sb(st, "wo%d" % i, [128, KC, NWD], BF16) for i in range(2)]
            g1bc = [sb(st, "g1bc%d" % m, [128, D], F32) for m in range(2)]
            xst = [sb(st, "xst%d" % i, [128, NWD], F32) for i in range(3)]
            t1 = [sb(st, "t1_%d" % i, [128, NWD], F32) for i in range(2)]
            t2 = [sb(st, "t2_%d" % i, [128, NWD], F32) for i in range(2)]
            xo = [sb(st, "xo%d" % i, [128, NWD], F32) for i in range(2)]
            for m in range(2):
                P.dma("sp", g1bc[m][:], bcast_row(MODROW[l][m:m + 1, 2 * D:3 * D]), writes=[("g1bc", m)], chan="g1bc%d" % m)
            it = 0
            ctiles = [(j, t) for j in range(ND4) for t in range(NT)]

            def xload(n):
                if n < len(ctiles):
                    j_, t_ = ctiles[n]
                    P.dma("sp", xst[n % 3][:], xsrc(t_ * 128, 128)[:, j_ * NWD:(j_ + 1) * NWD], writes=[("xst", n % 3)],
                          chan="xst%d" % (n % 3))
            xload(0)
            xload(1)
            for j in range(ND4):
                ws = j % 2
                P.dma("sp", wo[ws][:], WQO[l][j], reads=[("WQO", l, j)], writes=[("wo", ws)], chan="wo%d" % ws)
                for t in range(NT):
                    xload(it + 2)
                    m = 0 if t * 128 < NP else 1
                    xs_ = it % 3
                    i2 = it % 2
                    it += 1
                    bA, bC = (0, 1) if i2 == 0 else (2, 3)

                    def mmc(e, t=t, ws=ws, bA=bA, bC=bC):
                        for h in range(H):
                            e.matmul(psf[bA][:, 0:NWD], lhsT=OTg[:, h, t * 128:(t + 1) * 128], rhs=wo[ws][:, h, :],
                                     start=(h == 0), stop=(h == H - 1))
                        for c in range(CC):
                            ins = e.matmul(psf[bC][:, 0:NWD], lhsT=CTg[:, c, t * 128:(t + 1) * 128], rhs=wo[ws][:, H + c, :],
                                           start=(c == 0), stop=(c == CC - 1))
                        return ins
                    P.op("pe", mmc, reads=[("wo", ws)], writes=[PSK(bA), PSK(bC)])
                    P.op("act", lambda e, t=t, bA=bA, i2=i2: e.activation(out=t1[i2][:], in_=psf[bA][:, 0:NWD], func=AF.Copy,
                                                                          scale=ra[:, t:t + 1]), reads=[PSK(bA)], writes=[("t1", i2)])
                    P.op("dve", lambda e, t=t, bC=bC, i2=i2: e.scalar_tensor_tensor(out=t2[i2][:], in0=psf[bC][:, 0:NWD],
                                                                                    scalar=rc[:, t:t + 1], in1=t1[i2][:],
                                                                                    op0=ALU.mult, op1=ALU.add),
                         reads=[PSK(bC), ("t1", i2)], writes=[("t2", i2)])
                    P.op("dve", lambda e, i2=i2, m=m, j=j: e.tensor_tensor(out=t2[i2][:], in0=t2[i2][:],
                                                                           in1=g1bc[m][:, j * NWD:(j + 1) * NWD], op=ALU.mult),
                         reads=[("t2", i2), ("g1bc", m)], writes=[("t2", i2)])
                    P.op("dve", lambda e, i2=i2, xs_=xs_: e.tensor_tensor(out=xo[i2][:], in0=t2[i2][:], in1=xst[xs_][:], op=ALU.add),
                         reads=[("t2", i2), ("xst", xs_)], writes=[("xo", i2)])
                    P.dma("sp", XA[t * 128:(t + 1) * 128, j * NWD:(j + 1) * NWD], xo[i2][:], reads=[("xo", i2)], writes=[("XA", t, j)], chan="xo%d" % i2)
            P.flush()
            checkpoint()
        lst.close()

        with ExitStack() as st:
            psf, psbs = alloc_psum(st, 7, 1)
            gmT, shT = load_modT(st, l, 4, 3, g_norm2[l])
            xa = [sb(st, "xa%d" % i, [128, D], F32) for i in range(TPG)]
            xn = [sb(st, "xnD%d" % i, [128, D], BF16) for i in range(TPG)]
            junk = sb(st, "junkD", [128, D], BF16)
            ssq = sb(st, "ssqD", [128, 8], F32)
            r = sb(st, "rD", [128, 8], F32)
            h2T = sb(st, "h2T", [128, KC, GT], BF16)
            actT = sb(st, "actT", [128, FC, GT], BF16)
            wgu = [sb(st, "wgu%d" % i, [128, KC, 256], BF16) for i in range(3)]
            wdn = [sb(st, "wdn%d" % i, [128, FCH, NWD], BF16) for i in range(2)]
            g2bc = sb(st, "g2bc", [128, D], F32)
            sg = [sb(st, "sg%d" % i, [128, GT], F32) for i in range(2)]
            te = [sb(st, "te%d" % i, [128, NWD], F32) for i in range(2)]
            if last:
                gfb = sb(st, "gfb", [128, D], F32)
                P.dma("sp", gfb[:], bcast_row(g_fin.rearrange("(o n) -> o n", o=1)), writes=["gfb"], chan=P.uniq())
            gi = 0
            di = 0
            ei = 0
            do_ada = ADA_IN_FFN and (l + 1 < DEPTH)
            AW2 = 256 if D % 256 == 0 else D
            NAC = 6 * D // AW2
            ada_state = {"j": 0}
            if do_ada:
                wad2 = [sb(st, "wad2_%d" % i, [128, KC, AW2], BF16) for i in range(2)]
                brow2 = [sb(st, "brow2_%d" % i, [2, AW2], F32) for i in range(2)]
                mrow2 = [sb(st, "mrow2_%d" % i, [2, AW2], F32) for i in range(2)]

            def emit_ada_chunk():
                j = ada_state["j"]
                ada_state["j"] += 1
                s_ = j % 2
                ln = l + 1
                P.dma("pool", wad2[s_][:], w_ada[ln][:, j * AW2:(j + 1) * AW2].rearrange("(kc p) n -> p kc n", p=128),
                      writes=[("wad2", s_)], chan="wad2_%d" % s_)
                P.dma("pool", brow2[s_][:], b_ada[ln:ln + 1, j * AW2:(j + 1) * AW2].broadcast_to([2, AW2]),
                      writes=[("brow2", s_)], chan="brow2_%d" % s_)

                def mm(e):
                    for kc in range(KC):
                        ins = e.matmul(psf[6][0:2, 0:AW2], lhsT=scT[:, kc, :], rhs=wad2[s_][:, kc, :],
                                       start=(kc == 0), stop=(kc == KC - 1))
                    return ins
                P.op("pe", mm, reads=[("wad2", s_)], writes=[PSK(6)])
                P.op("dve", lambda e: e.tensor_tensor(out=mrow2[s_][:], in0=psf[6][0:2, 0:AW2], in1=brow2[s_][:], op=ALU.add),
                     reads=[PSK(6), ("brow2", s_)], writes=[("mrow2", s_)])
                P.dma("pool", MODROW[ln][:, j * AW2:(j + 1) * AW2], mrow2[s_][:], reads=[("mrow2", s_)],
                      writes=[("MODROW", ln, j)], chan="mrow2_%d" % s_)
            for gidx, (m, tok0) in enumerate(groups):
                P.dma("sp", g2bc[:], bcast_row(MODROW[l][m:m + 1, 5 * D:6 * D]), writes=["g2bc"], chan="g2bc")
                for i in range(TPG):
                    P.dma("sp", xa[i][:], XA[tok0 + i * 128:tok0 + (i + 1) * 128, :], writes=[("xa", i)], chan="xa%d" % i)
                norm_to_hT("D", xa, [("xa", i) for i in range(TPG)], gmT, shT, m, h2T, "h2T", xn, junk, ssq, r)
                for fc in range(FC):
                    s = gi % 3
                    pb = (gi % 2) * 2
                    gi += 1
                    P.dma("sp", wgu[s][:], WGU[l][fc], reads=[("WGU", l, fc, 0), ("WGU", l, fc, 1)], writes=[("wgu", s)], chan="wgu%d" % s)

                    def mg(e, s=s, pb=pb):
                        for half in range(2):
                            for kc in range(KC):
                                ins = e.matmul(psf[pb + half][:, 0:GT], lhsT=wgu[s][:, kc, half * 128:(half + 1) * 128],
                                               rhs=h2T[:, kc, :], start=(kc == 0), stop=(kc == KC - 1))
                        return ins
                    P.op("pe", mg, reads=[("wgu", s), "h2T"], writes=[PSK(pb), PSK(pb + 1)])
                    s2 = fc % 2
                    P.op("act", lambda e, pb=pb, s2=s2: e.activation(out=sg[s2][:], in_=psf[pb][:, 0:GT], func=AF.Silu),
                         reads=[PSK(pb)], writes=[("sg", s2)])
                    P.op("dve", lambda e, pb=pb, s2=s2, fc=fc: e.tensor_tensor(out=actT[:, fc, :], in0=sg[s2][:],
                                                                               in1=psf[pb + 1][:, 0:GT], op=ALU.mult),
                         reads=[PSK(pb + 1), ("sg", s2)], writes=["actT"])
                    if do_ada:
                        quota = ((gidx + 1) * NAC + len(groups) - 1) // len(groups)
                        every = max(1, FC // max(1, (NAC + len(groups) - 1) // len(groups)))
                        if (fc % every == every - 1 or fc == FC - 1) and ada_state["j"] < min(quota, NAC):
                            emit_ada_chunk()
                            while fc == FC - 1 and ada_state["j"] < min(quota, NAC):
                                emit_ada_chunk()
                for j in range(ND4):
                    for q in range(FQ):
                        s = di % 2
                        di += 1
                        P.dma("sp", wdn[s][:], WDN[l][j][q], reads=[("WDN", l, j, q)], writes=[("wdn", s)], chan="wdn%d" % s)
                        for i in range(TPG):
                            yb = 2 + i

                            def md(e, s=s, q=q, i=i, yb=yb):
                                for f in range(FCH):
                                    ins = e.matmul(psf[yb][:, 0:NWD], lhsT=actT[:, q * FCH + f, i * 128:(i + 1) * 128],
                                                   rhs=wdn[s][:, f, :], start=(q == 0 and f == 0),
                                                   stop=(q == FQ - 1 and f == FCH - 1))
                                return ins
                            P.op("pe", md, reads=[("wdn", s), "actT"], writes=[PSK(yb)])
                    for i in range(TPG):
                        yb = 2 + i
                        e2 = ei % 2
                        ei += 1
                        P.op("dve", lambda e, yb=yb, e2=e2, j=j: e.tensor_tensor(out=te[e2][:], in0=psf[yb][:, 0:NWD],
                                                                                 in1=g2bc[:, j * NWD:(j + 1) * NWD], op=ALU.mult),
                             reads=[PSK(yb), "g2bc"], writes=[("te", e2)])
                        P.op("dve", lambda e, e2=e2, i=i, j=j: e.tensor_tensor(out=xa[i][:, j * NWD:(j + 1) * NWD], in0=te[e2][:],
                                                                               in1=xa[i][:, j * NWD:(j + 1) * NWD], op=ALU.add),
                             reads=[("te", e2), ("xa", i)], writes=[("xa", i)])
                for i in range(TPG):
                    t0 = tok0 + i * 128
                    if not last:
                        P.dma("sp", XB[t0:t0 + 128, :], xa[i][:], reads=[("xa", i)], writes=[("XB", t0)], chan="xa%d" % i)
                    else:
                        P.op("act", lambda e, i=i: e.activation(out=junk[:], in_=xa[i][:], func=AF.Square,
                                                                accum_out=ssq[:, 4 + i:5 + i]),
                             reads=[("xa", i)], writes=[("fs", i)])
                        rstd_ops(ssq[:, 4 + i:5 + i], r[:, 4 + i:5 + i], D, ("fs", i), ("fr", i))
                        P.op("dve", lambda e, i=i: e.scalar_tensor_tensor(out=xa[i][:], in0=xa[i][:], scalar=r[:, 4 + i:5 + i],
                                                                          in1=gfb[:], op0=ALU.mult, op1=ALU.mult),
                             reads=[("xa", i), ("fr", i), "gfb"], writes=[("xa", i)])
                        dst = yp[t0:t0 + 128, :] if t0 < NP else ys[t0 - NP:t0 - NP + 128, :]
                        P.dma("sp", dst, xa[i][:], reads=[("xa", i)], writes=[("yout", t0)], chan="xa%d" % i)
                P.flush()
                checkpoint()
    stack.close()


_CACHE = {}
_LAST = None


def _run(cfg, ins):
    key = (cfg.D, cfg.H, cfg.SS, cfg.FF)
    if key not in _CACHE:
        _CACHE[key] = build_program(cfg)
    nc = _CACHE[key]
    f = lambda a: np.ascontiguousarray(a, dtype=np.float32)
    D, NPB, SP, SS = cfg.D, cfg.NPB, cfg.SP, cfg.SS
    AW = cfg.H * 128
    shared = {k: f(ins[k]) for k in ("w_ada", "b_ada", "g_norm1", "w_in", "g_kv", "w_conv", "g_attn_out", "g_conv_out",
                                     "w_out", "g_norm2", "w_gate_up", "w_down", "g_final")}
    shared["w_uk"] = f(ins["w_uk"]).reshape(cfg.DEPTH, 256, AW)
    shared["w_uv"] = f(ins["w_uv"]).reshape(cfg.DEPTH, 256, AW)
    in_maps = []
    for c in range(8):
        b, hf = c // 2, c % 2
        tok = np.arange(hf * SS, (hf + 1) * SS)
        posa = np.stack([tok // cfg.GRID_W, tok % cfg.GRID_W], axis=1).astype(np.float32)
        mk = np.zeros((128, 2), np.float32)
        mk[:, 0] = 1.0 if hf == 1 else 0.0
        mk[:, 1] = 1.0 if hf == 0 else 0.0
        d = dict(shared)
        d["xp"] = f(ins["x_prompt"][c * NPB:(c + 1) * NPB].reshape(NPB * SP, D))
        d["xs"] = f(ins["x_sample"][b, hf * SS:(hf + 1) * SS])
        d["cckv"] = f(ins["cache_ckv"][b])
        d["ckr"] = f(ins["cache_krope"][b])
        d["cvec"] = f(np.stack([ins["c_ctx"], ins["c"][b]], axis=0))
        d["pos"] = posa
        d["msk"] = mk
        in_maps.append(d)
    res = run_bass_kernel_spmd(nc, in_maps, core_ids=list(range(8)))
    R = res.results
    global _LAST
    _LAST = R
    y_p = np.concatenate([R[c]["yp"].reshape(NPB, SP, D) for c in range(8)], axis=0)
    y_s = np.stack([np.concatenate([R[2 * b]["ys"], R[2 * b + 1]["ys"]], axis=0) for b in range(4)], axis=0)
    n_ckv = np.concatenate([R[c]["nckv"] for c in range(8)], axis=0)
    n_kr = np.concatenate([R[c]["nkr"] for c in range(8)], axis=0)
    return (y_p.astype(np.float32), y_s.astype(np.float32), n_ckv.astype(np.float32), n_kr.astype(np.float32))


def kernel(**inputs):
    return _run(Cfg(), {k: np.asarray(v) for k, v in inputs.items()})
```
